# Optimizing a Trainium2 kernel written in Bass

```python
import jax, jax.numpy as jnp
from jax import lax
import numpy as np

D_MODEL = 2048
BATCH = 1
SEQ = 16384
DEPTH = 4
DEC_BATCH = 8
DEC_SEQ = 64
PAST_LEN = 4096

CHUNK = 64
D_MIX = D_MODEL
GLA_WIDTH = D_MIX // 2
GLA_HEADS = 4
GLA_DV = GLA_WIDTH // GLA_HEADS
GLA_DK = GLA_DV // 2
GLA_KEY_WIDTH = GLA_HEADS * GLA_DK
GATE_RANK = 16
GATE_TEMP = 16.0
MLP_WIDTH = D_MIX - GLA_WIDTH
MLP_GROUPS = 4
MLP_GC = MLP_WIDTH // MLP_GROUPS
MLP_CHUNK = 128
D_IN = 2 * GLA_KEY_WIDTH + 2 * GLA_WIDTH + GATE_RANK + 3 * MLP_WIDTH
EPS = 1e-6

kernel_name = "hymba_gla_gmlp_streaming_step"


def rmsnorm(x, g):
    xf = x.astype(jnp.float32)
    y = xf * lax.rsqrt(jnp.mean(xf * xf, axis=-1, keepdims=True) + EPS)
    return (y * g.astype(jnp.float32)).astype(x.dtype)


def layernorm(x, g, b):
    xf = x.astype(jnp.float32)
    mu = jnp.mean(xf, axis=-1, keepdims=True)
    xc = xf - mu
    y = xc * lax.rsqrt(jnp.mean(xc * xc, axis=-1, keepdims=True) + EPS)
    return (y * g.astype(jnp.float32) + b.astype(jnp.float32)).astype(x.dtype)


def split_projection(z):
    sizes = [GLA_KEY_WIDTH, GLA_KEY_WIDTH, GLA_WIDTH, GLA_WIDTH, GATE_RANK, MLP_WIDTH, MLP_WIDTH, MLP_WIDTH]
    outs, off = [], 0
    for s in sizes:
        outs.append(z[..., off:off + s])
        off += s
    return outs


def gla_scan(q, k, v, log_a, s0, blk):
    B, L, H, _ = q.shape
    n = L // blk

    def to_blocks(t):
        return t.astype(jnp.float32).reshape(B, n, blk, H, t.shape[-1]).transpose(1, 0, 3, 2, 4)

    causal = jnp.tril(jnp.ones((blk, blk), dtype=bool))[:, :, None]

    def step(S, inp):
        qb, kb, vb, gb = inp
        b = jnp.cumsum(gb, axis=-2)
        diff = b[..., :, None, :] - b[..., None, :, :]
        decay = jnp.where(causal, jnp.exp(jnp.where(causal, diff, 0.0)), 0.0)
        scores = jnp.einsum('bhtd,bhsd,bhtsd->bhts', qb, kb, decay)
        o = (jnp.einsum('bhts,bhsv->bhtv', scores, vb)
             + jnp.einsum('bhtd,bhdv->bhtv', qb * jnp.exp(b), S))
        b_last = b[..., -1:, :]
        S_new = (jnp.exp(b_last[..., 0, :])[..., None] * S
                 + jnp.einsum('bhsd,bhsv->bhdv', kb * jnp.exp(b_last - b), vb))
        return S_new, o

    S, o = lax.scan(step, s0, (to_blocks(q), to_blocks(k), to_blocks(v), to_blocks(log_a)))
    o = o.transpose(1, 0, 3, 2, 4).reshape(B, L, H, v.shape[-1])
    return o, S


def spatial_gate(v, w_s, b_s, blk):
    B, L, G, C = v.shape
    n = L // blk
    wm = jnp.where(jnp.tril(jnp.ones((blk, blk), dtype=bool))[None], w_s[:, :blk, :blk], 0.0)
    vb = v.reshape(B, n, blk, G, C)
    s = jnp.einsum('gts,bnsgc->bntgc', wm.astype(v.dtype), vb) + b_s[:, :blk].T[None, None, :, :, None]
    return s.reshape(B, L, G, C)


def layer(x, s0, gla_blk, w_in, w_gate_up, b_gate, w_s, b_s, norm_g, gla_norm_g, mlp_ln_g, mlp_ln_b, w_out):
    B, L, _ = x.shape
    h = rmsnorm(x, norm_g)
    z = jnp.einsum('bld,de->ble', h, w_in)
    q, k, v, g_a, lr, u, vm, g_b = split_projection(z)
    log_a = jax.nn.log_sigmoid((jnp.einsum('blr,rk->blk', lr, w_gate_up) + b_gate).astype(jnp.float32)) / GATE_TEMP
    q = (q * (GLA_DK ** -0.5)).reshape(B, L, GLA_HEADS, GLA_DK)
    k = k.reshape(B, L, GLA_HEADS, GLA_DK)
    v = v.reshape(B, L, GLA_HEADS, GLA_DV)
    log_a = log_a.reshape(B, L, GLA_HEADS, GLA_DK)
    o, S = gla_scan(q, k, v, log_a, s0, gla_blk)
    o_a = rmsnorm(o.astype(x.dtype), gla_norm_g).reshape(B, L, GLA_WIDTH) * jax.nn.silu(g_a)
    vm_n = layernorm(vm.reshape(B, L, MLP_GROUPS, MLP_GC),
                     mlp_ln_g.reshape(MLP_GROUPS, MLP_GC), mlp_ln_b.reshape(MLP_GROUPS, MLP_GC))
    sg = spatial_gate(vm_n, w_s, b_s, min(L, MLP_CHUNK)).reshape(B, L, MLP_WIDTH)
    o_b = u * sg * jax.nn.silu(g_b)
    y = x + jnp.einsum('ble,ed->bld', jnp.concatenate([o_a, o_b], axis=-1), w_out)
    return y, S, vm_n.reshape(B, L, MLP_WIDTH)


def setup_inputs(seed: int = 0) -> dict:
    key = jax.random.key(seed)
    ks = jax.random.split(key, 16)
    f32 = jnp.float32
    return {
        "x_prompt": jax.random.normal(ks[0], (BATCH, SEQ, D_MODEL), f32),
        "x_sample": jax.random.normal(ks[1], (DEC_BATCH, DEC_SEQ, D_MODEL), f32),
        "state_gla": 0.5 * jax.random.normal(ks[2], (DEPTH, DEC_BATCH, GLA_HEADS, GLA_DK, GLA_DV), f32),
        "w_in": jax.random.normal(ks[3], (DEPTH, D_MODEL, D_IN), f32) * D_MODEL ** -0.5,
        "w_gate_up": jax.random.normal(ks[4], (DEPTH, GATE_RANK, GLA_KEY_WIDTH), f32) * GATE_RANK ** -0.5,
        "b_gate": 0.1 * jax.random.normal(ks[5], (DEPTH, GLA_KEY_WIDTH), f32),
        "w_s": jax.random.normal(ks[6], (DEPTH, MLP_GROUPS, MLP_CHUNK, MLP_CHUNK), f32) * MLP_CHUNK ** -0.5,
        "b_s": 1.0 + 0.1 * jax.random.normal(ks[7], (DEPTH, MLP_GROUPS, MLP_CHUNK), f32),
        "norm_g": 1.0 + 0.05 * jax.random.normal(ks[8], (DEPTH, D_MODEL), f32),
        "gla_norm_g": 1.0 + 0.05 * jax.random.normal(ks[9], (DEPTH, GLA_DV), f32),
        "mlp_ln_g": 1.0 + 0.05 * jax.random.normal(ks[10], (DEPTH, MLP_WIDTH), f32),
        "mlp_ln_b": 0.05 * jax.random.normal(ks[11], (DEPTH, MLP_WIDTH), f32),
        "w_out": jax.random.normal(ks[12], (DEPTH, D_MIX, D_MODEL), f32) * D_MIX ** -0.5,
        "final_norm_g": 1.0 + 0.05 * jax.random.normal(ks[13], (D_MODEL,), f32),
    }


def reference(x_prompt, x_sample, state_gla, w_in, w_gate_up, b_gate, w_s, b_s, norm_g, gla_norm_g,
              mlp_ln_g, mlp_ln_b, w_out, final_norm_g):
    yp, ys = x_prompt, x_sample
    gla_p, gla_s, v_s = [], [], []
    for l in range(DEPTH):
        params = (w_in[l], w_gate_up[l], b_gate[l], w_s[l], b_s[l], norm_g[l], gla_norm_g[l],
                  mlp_ln_g[l], mlp_ln_b[l], w_out[l])
        s0_p = jnp.zeros((yp.shape[0], GLA_HEADS, GLA_DK, GLA_DV), jnp.float32)
        yp, Sp, _ = layer(yp, s0_p, CHUNK, *params)
        ys, Ss, vs = layer(ys, state_gla[l].astype(jnp.float32), ys.shape[1], *params)
        gla_p.append(Sp.astype(x_prompt.dtype))
        gla_s.append(Ss.astype(state_gla.dtype))
        v_s.append(vs)
    y_prompt = rmsnorm(yp, final_norm_g)
    y_sample = rmsnorm(ys, final_norm_g)
    gla_state_prompt = jnp.stack(gla_p)
    gla_state_sample = jnp.stack(gla_s)
    mlp_v_sample = jnp.stack(v_s)
    return (y_prompt, y_sample, gla_state_prompt, gla_state_sample, mlp_v_sample)
```

```python
import numpy as np
from contextlib import ExitStack
import concourse.bass as bass
import concourse.mybir as mybir
from concourse.bass_utils import run_bass_kernel_spmd

F32 = mybir.dt.float32
BF16 = mybir.dt.bfloat16
AF = mybir.ActivationFunctionType
ALU = mybir.AluOpType

NCORES = 8
D = 2048
DIN = 6160
NH = 4
DK = 128
DV = 256
TP = 512
NBP = 4
SSQ = 64
TT = TP + SSQ
EPS = 1e-6
ENGS = ("pe", "act", "dve", "pool", "sp")
DEBUG_STAGE = None
DEBUG_SUB = 99


class _Stop(Exception):
    pass


def _stage(n):
    if DEBUG_STAGE is not None and DEBUG_STAGE < n:
        raise _Stop()


class _Op:
    __slots__ = ("eng", "fn", "deps", "signal", "lane", "count", "inc")

    def __init__(self, eng, fn, lane=None, inc=16):
        self.eng = eng
        self.fn = fn
        self.deps = []
        self.signal = False
        self.lane = lane
        self.count = None
        self.inc = inc


class Sched:
    def __init__(self):
        self.ops = []
        self.last_writer = {}
        self.readers = {}
        self.lane_last = {}
        self.lanes = []

    def _add(self, op, reads, writes):
        deps = []
        excl = [r for r in reads if r.startswith("ps")]
        reads = [r for r in reads if not r.startswith("ps")]
        writes = list(writes) + excl
        for r in reads:
            w = self.last_writer.get(r)
            if w is not None:
                deps.append(w)
        for r in writes:
            w = self.last_writer.get(r)
            if w is not None:
                deps.append(w)
            deps.extend(self.readers.get(r, ()))
        if op.lane is not None:
            p = self.lane_last.get(op.lane)
            if p is not None:
                deps.append(p)
            else:
                self.lanes.append(op.lane)
            self.lane_last[op.lane] = op
        seen = set()
        for d in deps:
            if id(d) in seen or d is op:
                continue
            seen.add(id(d))
            if d.lane is None and op.lane is None and d.eng == "pe" and op.eng == "pe":
                continue
            op.deps.append(d)
            d.signal = True
        for r in reads:
            self.readers.setdefault(r, []).append(op)
        for r in writes:
            self.last_writer[r] = op
            self.readers[r] = []
        self.ops.append(op)
        return op

    def op(self, eng, fn, reads=(), writes=()):
        return self._add(_Op(eng, fn), list(reads), list(writes))

    def dma(self, eng, lane, fn, reads=(), writes=(), inc=16):
        op = _Op(eng, fn, lane=lane, inc=inc)
        op.signal = True
        return self._add(op, list(reads), list(writes))

    def emit(self, nc):
        cnt = {e: 0 for e in ENGS}
        lane_cnt = {}
        for op in self.ops:
            if op.lane is not None:
                lane_cnt[op.lane] = lane_cnt.get(op.lane, 0) + op.inc
                op.count = lane_cnt[op.lane]
            elif op.signal:
                cnt[op.eng] += 1
                op.count = cnt[op.eng]
        with ExitStack() as st:
            esem = {e: st.enter_context(nc.semaphore("sem_" + e)) for e in ENGS}
            lsem = {l: st.enter_context(nc.semaphore("lane_%d" % i)) for i, l in enumerate(self.lanes)}
            block = st.enter_context(nc.Block())
            streams = {e: [op for op in self.ops if op.eng == e] for e in ENGS}

            def run(e, eng):
                seen = {}
                for op in streams[e]:
                    for d in op.deps:
                        sem = lsem[d.lane] if d.lane is not None else esem[d.eng]
                        if seen.get(id(sem), 0) >= d.count:
                            continue
                        seen[id(sem)] = d.count
                        eng.wait_ge(sem, d.count)
                    ins = op.fn(eng)
                    if op.lane is not None:
                        if op.inc == 16:
                            ins.then_inc(lsem[op.lane], 16)
                        else:
                            ins.then_inc(lsem[op.lane])
                    elif op.signal:
                        ins.then_inc(esem[op.eng], 1)
                if e == "sp":
                    for l in self.lanes:
                        eng.wait_ge(lsem[l], lane_cnt[l])

            block.tensor(lambda eng: run("pe", eng))
            block.scalar(lambda eng: run("act", eng))
            block.vector(lambda eng: run("dve", eng))
            block.gpsimd(lambda eng: run("pool", eng))
            block.sync(lambda eng: run("sp", eng))


def build(L, R):
    nc = bass.Bass("TRN2", target_bir_lowering=False)

    def din(name, shape):
        return nc.dram_tensor(name, shape, F32, kind="ExternalInput").ap()

    def dout(name, shape):
        return nc.dram_tensor(name, shape, F32, kind="ExternalOutput").ap()

    xp = din("xp", [R, TP, D]); xs = din("xs", [SSQ, D]); sg = din("sg", [L, NH, DK, DV])
    w_in = din("w_in", [L, D, DIN]); w_out = din("w_out", [L, D, D])
    w_gu = din("w_gate_up", [L, 16, 512]); b_gate = din("b_gate", [L, 512])
    w_s = din("w_s", [L, 4, 128, 128]); b_s = din("b_s", [L, 4, 128])
    norm_g = din("norm_g", [L, D]); gla_g = din("gla_norm_g", [L, DV])
    ln_g = din("mlp_ln_g", [L, 1024]); ln_b = din("mlp_ln_b", [L, 1024]); fin_g = din("final_norm_g", [D])
    ident = din("ident", [128, 128]); triu = din("triu", [128, 128]); onehot = din("onehot", [128, NCORES])
    yp = dout("yp", [R, TP, D]); ys = dout("ys", [SSQ, D])
    gsp = dout("gsp", [L, NH, DK, DV]); gss = dout("gss", [L, NH, DK, DV]); mvo = dout("mv", [L, SSQ, 1024])
    resid_p = nc.dram_tensor("resid_p", [R, TP, D], F32).ap()
    resid_s = nc.dram_tensor("resid_s", [SSQ, D], F32).ap()
    cc_in = [nc.dram_tensor("cc_in%d" % i, [128, 1028], F32) for i in range(2)]
    cc_out = [nc.dram_tensor("cc_out%d" % i, [NCORES * 128, 1028], F32) for i in range(2)]

    S = Sched()
    with ExitStack() as st:
        def sb(name, shape, dt):
            return st.enter_context(nc.sbuf_tensor(name, shape, dt))

        xt = sb("xt", [128, 2, D], F32)
        xn = sb("xn", [128, D], BF16)
        hT = sb("hT", [128, 16, TT], BF16)
        slab = sb("slab", [128, 2, 16, 512], BF16)
        slab_lr = sb("slab_lr", [128, 16, 16], BF16)
        wst = sb("wst", [128, 4, 2, 512], F32)
        wst_lr = sb("wst_lr", [128, 16, 16], F32)
        qk = sb("qk", [128, 2, NH, TT], BF16)
        oT = sb("oT", [128, 16, TT], BF16)
        lrT = sb("lrT", [17, TT], F32)
        wg = sb("wg", [17, 512], F32)
        Ee = sb("Ee", [128, 2, NH, 128], F32)
        khT = sb("khT", [128, NH, 128], BF16)
        khat = sb("khat", [128, 512], BF16)
        vv = sb("vv", [128, NBP + 1, 1024], BF16)
        Sloc = sb("Sloc", [128, 1024], F32)
        Slocbf = sb("Slocbf", [128, NBP - 1, 1024], BF16)
        Pall = sb("Pall", [128, NBP + 1, NH], F32)
        Gt2 = sb("Gt2", [128, 2, 1028], F32)
        Scarry = sb("Scarry", [128, 1024], F32)
        Sstart = sb("Sstart", [128, 1024], F32)
        Ssamp = sb("Ssamp", [128, 1024], F32)
        Sinbf = sb("Sinbf", [128, 1024], BF16)
        scm = sb("scm", [128, NH, 128], BF16)
        Mt = sb("Mt", [128, NBP + 1, 1024], BF16)
        vmn = sb("vmn", [128, NBP + 1, 1024], BF16)
        lnp = sb("lnp", [128, 2, 1024], F32)
        g2x2 = sb("g2x2", [128, 512], F32)
        wsT = sb("wsT", [128, 4, 128], BF16)
        bsT = sb("bsT", [128, 4], F32)
        g16 = sb("g16", [128, 16], F32)
        tmpf = sb("tmpf", [128, 2, 512], F32)
        wsf = tmpf[:, 0, :].rearrange("p (g s) -> p g s", g=4)
        spt = tmpf[:, 1, :]
        junk = sb("junk", [128, 256], BF16)
        idf = sb("idf", [128, 128], F32); idb = sb("idb", [128, 128], BF16)
        triuf = sb("triuf", [128, 128], F32); maskb = sb("maskb", [128, NH, 128], BF16)
        oneh = sb("oneh", [128, NCORES], F32)
        epsc = sb("epsc", [128, 1], F32)
        sm = sb("sm", [128, 32], F32)
        ps = st.enter_context(nc.psum_tensor("ps", [128, 8, 512], F32))

        cntA = [0]; cntB = [0]

        def bankA():
            k = cntA[0] % 4; cntA[0] += 1
            return k

        def bankB():
            k = 4 + 2 * (cntB[0] % 2); cntB[0] += 1
            return k

        def psbf(k):
            return ps[:, k, :].bitcast(BF16)

        def ACT(out, in_, func, reads, writes, **kw):
            S.op("act", lambda e: e.activation(out=out, in_=in_, func=func, **kw), reads, writes)

        def TT_(eng, out, in0, in1, op, reads, writes):
            S.op(eng, lambda e: e.tensor_tensor(out=out, in0=in0, in1=in1, op=op), reads, writes)

        def STT(eng, out, in0, scalar, in1, op0, op1, reads, writes):
            S.op(eng, lambda e: e.scalar_tensor_tensor(out=out, in0=in0, scalar=scalar, in1=in1, op0=op0, op1=op1), reads, writes)

        def TS(eng, out, in0, s1, s2, op0, op1, reads, writes):
            if s2 is None:
                S.op(eng, lambda e: e.tensor_scalar(out=out, in0=in0, scalar1=s1, scalar2=None, op0=op0), reads, writes)
            else:
                S.op(eng, lambda e: e.tensor_scalar(out=out, in0=in0, scalar1=s1, scalar2=s2, op0=op0, op1=op1), reads, writes)

        def CP(eng, out, in_, reads, writes):
            if eng == "act":
                S.op("act", lambda e: e.copy(out=out, in_=in_), reads, writes)
            else:
                S.op(eng, lambda e: e.tensor_copy(out=out, in_=in_), reads, writes)

        def MM(out, lhsT, rhs, start, stop, reads, writes):
            S.op("pe", lambda e: e.matmul(out, lhsT=lhsT, rhs=rhs, start=start, stop=stop), reads, writes)

        def TR(out, in_, idn, reads, writes):
            S.op("pe", lambda e: e.transpose(out=out, in_=in_, identity=idn), reads, writes)

        def DMA(eng, lane, out, in_, reads, writes, slow=False):
            if slow:
                S.dma(eng, lane, lambda e: e.dma_start(out=out, in_=in_, allow_slow_non_contiguous=True), reads, writes)
            else:
                S.dma(eng, lane, lambda e: e.dma_start(out=out, in_=in_), reads, writes)

        def rsqrt_(ap, n, scale, reads_writes):
            ACT(ap, ap, AF.Ln, reads_writes + ["epsc"], reads_writes, scale=scale, bias=epsc[0:n, 0:1])
            ACT(ap, ap, AF.Exp, reads_writes, reads_writes, scale=-0.5)

        DMA("sp", "c0", idf[:, :], ident[:, :], [], ["idf"])
        DMA("sp", "c1", triuf[:, :], triu[:, :], [], ["triuf"])
        DMA("sp", "c2", oneh[:, :], onehot[:, :], [], ["oneh"])
        CP("dve", idb[:, :], idf[:, :], ["idf"], ["idb"])
        for h in range(NH):
            CP("dve", maskb[:, h, :], triuf[:, :], ["triuf"], ["maskb"])
        S.op("dve", lambda e: e.memset(epsc[:, :], EPS), [], ["epsc"])
        S.op("dve", lambda e: e.memset(lrT[:, :], 1.0), [], ["lrT"])
        S.op("dve", lambda e: e.memset(Pall[:, :, :], 1.0), [], ["Pall"])

        def make_blocks(l, r):
            blocks = []
            for j in range(NBP):
                src = (xp if l == 0 else resid_p)[r, j * 128:(j + 1) * 128, :]
                blocks.append(dict(b=j, nb=128, off=j * 128, src=src, dst=resid_p[r, j * 128:(j + 1) * 128, :],
                                   res="res_%d_%d" % (r, j), samp=False))
            if r == 0:
                blocks.append(dict(b=NBP, nb=SSQ, off=TP, src=(xs if l == 0 else resid_s), dst=resid_s,
                                   res="res_s", samp=True))
            return blocks

        def phaseA(l, r):
            if r == 0:
                DMA("sp", "p0", g16[:, :], norm_g[l].rearrange("(c p) -> p c", p=128), [], ["g16"], slow=True)
            for bi, blk in enumerate(make_blocks(l, r)):
                nb, off, b = blk["nb"], blk["off"], blk["b"]
                k = bi % 2
                xres = ["xtq%d" % (4 * k + q) for q in range(4)]
                DMA("sp", "xt%d" % k, xt[0:nb, k, :], blk["src"], [blk["res"] + "_%d" % q for q in range(4)], xres)
                ACT(xn[0:nb, :], xt[0:nb, k, :], AF.Square, xres, ["xn", "sm0"], accum_out=sm[0:nb, 0:1])
                rsqrt_(sm[0:nb, 0:1], nb, 1.0 / D, ["sm0"])
                ACT(xn[0:nb, :], xt[0:nb, k, :], AF.Copy, xres + ["sm0"], ["xn"], scale=sm[0:nb, 0:1])
                for half in range(2):
                    kb = bankA()
                    pv = psbf(kb).rearrange("p (j t) -> p j t", j=8)
                    for j in range(8):
                        c = half * 8 + j
                        TR(pv[:, j, 0:nb], xn[0:nb, c * 128:(c + 1) * 128], idb[0:nb, 0:nb], ["xn", "idb"], ["ps%d" % kb])
                    TT_("dve", hT[:, half * 8:(half + 1) * 8, off:off + nb], pv[:, :, 0:nb],
                        g16[:, half * 8:(half + 1) * 8].unsqueeze(2).to_broadcast([128, 8, nb]), ALU.mult,
                        ["ps%d" % kb, "g16"], ["hT%d" % b])

        tiles = [(l_, r_) for l_ in range(L) for r_ in range(R)]
        doneA = set()

        slab_cnt = [0]
        pending_slabs = []

        wst_cnt = [0]
        cast_engs = ["act", "dve", "act", "dve"]

        def slab_issue(l, kind, c0):
            k = slab_cnt[0] % 2; slab_cnt[0] += 1
            if kind == "in":
                src = w_in[l].rearrange("(c p) e -> p c e", p=128)
            else:
                src = w_out[l].rearrange("(c p) e -> p c e", p=128)
            for q in range(8):
                j = wst_cnt[0] % 4; wst_cnt[0] += 1
                DMA("sp", "wst%d" % j, wst[:, j, :, :], src[:, 2 * q:2 * q + 2, c0:c0 + 512], [], ["wst%d" % j])
                CP(cast_engs[q % 4], slab[:, k, 2 * q:2 * q + 2, :], wst[:, j, :, :], ["wst%d" % j], ["slab%d_%d" % (k, q)])
            return k

        cc_count = [0]

        try:
          for l in range(L):
              DMA("sp", "p1", wg[0:16, :], w_gu[l], [], ["wg"])
              DMA("sp", "p1", wg[16:17, :], b_gate[l:l + 1, :], [], ["wg"])
              DMA("sp", "p2", lnp[:, 0, :], ln_g[l].partition_broadcast(128), [], ["lnp"])
              DMA("sp", "p2", lnp[:, 1, :], ln_b[l].partition_broadcast(128), [], ["lnp"])
              DMA("sp", "p3", g2x2[:, 0:256], gla_g[l].partition_broadcast(128), [], ["g2x2"])
              DMA("sp", "p3", g2x2[:, 256:512], gla_g[l].partition_broadcast(128), [], ["g2x2"])
              DMA("sp", "p4", wsf[:, :, :], w_s[l].rearrange("g t s -> t g s"), [], ["tmpf0"])
              DMA("sp", "p5", bsT[:, :], b_s[l].rearrange("g t -> t g"), [], ["bsT"], slow=True)
              DMA("sp", "p6", Ssamp[:, :].rearrange("p (h v) -> p h v", h=NH), sg[l].rearrange("h d v -> d h v"), [], ["Ssamp"])
              kb = bankA()
              for g in range(4):
                  TR(ps[:, kb, g * 128:(g + 1) * 128], wsf[:, g, :], idf[:, :], ["tmpf0", "idf"], ["ps%d" % kb])
              TT_("dve", wsT[:, :, :], ps[:, kb, :].rearrange("p (g t) -> p g t", g=4), maskb[:, :, :], ALU.mult,
                  ["ps%d" % kb, "maskb"], ["wsT"])
              S.op("dve", lambda e: e.memset(Scarry[:, :], 0.0), [], ["Scarry"])

              for r in range(R):
                  blocks = make_blocks(l, r)
                  pblocks = [b for b in blocks if not b["samp"]]
                  tok_ranges = [(0, TP)] + ([(TP, SSQ)] if r == 0 else [])

                  sched = [("in", 0), ("in", 512), ("in", 1024), ("in", 1536), ("in", 2048), ("in", 2560),
                           ("in", 4112), ("in", 4624), ("in", 3088), ("in", 3600), ("in", 5136), ("in", 5648),
                           ("out", 0), ("out", 512), ("out", 1024), ("out", 1536)]
                  issued = []

                  def get_slab(i):
                      while len(issued) < min(len(sched), i + 2):
                          kind, c0 = sched[len(issued)]
                          issued.append(slab_issue(l, kind, c0))
                      return issued[i]

                  DMA("sp", "slr", wst_lr[:, :, :], w_in[l].rearrange("(c p) e -> p c e", p=128)[:, :, 3072:3088],
                      [], ["wst_lr"], slow=True)
                  CP("dve", slab_lr[:, :, :], wst_lr[:, :, :], ["wst_lr"], ["slab_lr"])
                  get_slab(0)

                  _stage(2)
                  if (l, r) not in doneA:
                      doneA.add((l, r))
                      phaseA(l, r)
                  hTall = ["hT%d" % blk["b"] for blk in blocks]

                  def form1(si, evac):
                      k = get_slab(si)
                      pending = None
                      for blk in blocks:
                          nb, off, b = blk["nb"], blk["off"], blk["b"]
                          kb = bankA()
                          for c in range(16):
                              MM(ps[0:nb, kb, :], hT[:, c, off:off + nb], slab[:, k, c, :], c == 0, c == 15,
                                 ["hT%d" % b, "slab%d_%d" % (k, c // 2)], ["ps%d" % kb])
                          later = evac(blk, kb)
                          if pending is not None:
                              pending()
                          pending = later
                      if pending is not None:
                          pending()

                  _stage(3)
                  for (t0, tn) in tok_ranges:
                      kb = bankA()
                      for c in range(16):
                          MM(ps[0:16, kb, 0:tn], slab_lr[:, c, :], hT[:, c, t0:t0 + tn], c == 0, c == 15,
                             hTall + ["slab_lr"], ["ps%d" % kb])
                      CP("act", lrT[0:16, t0:t0 + tn], ps[0:16, kb, 0:tn], ["ps%d" % kb], ["lrT"])
                  for which in range(2):
                      k = get_slab(which)
                      for (t0, tn) in tok_ranges:
                          for h in range(NH):
                              kb = bankA()
                              for c in range(16):
                                  MM(ps[:, kb, 0:tn], slab[:, k, c, h * 128:(h + 1) * 128], hT[:, c, t0:t0 + tn], c == 0, c == 15,
                                     hTall + ["slab%d_%d" % (k, c // 2)], ["ps%d" % kb])
                              if which == 0:
                                  ACT(qk[:, 0, h, t0:t0 + tn], ps[:, kb, 0:tn], AF.Copy, ["ps%d" % kb], ["qk"], scale=DK ** -0.5)
                              else:
                                  CP("dve", qk[:, 1, h, t0:t0 + tn], ps[:, kb, 0:tn], ["ps%d" % kb], ["qk"])

                  _stage(4)
                  def evac_v(half):
                      def f(blk, kb):
                          nb, b = blk["nb"], blk["b"]
                          CP("act", vv[0:nb, b, half * 512:(half + 1) * 512], ps[0:nb, kb, :], ["ps%d" % kb], ["vv%d" % b])
                      return f
                  form1(2, evac_v(0))
                  form1(3, evac_v(1))

                  def gla_prep(blk):
                      nb, off, b = blk["nb"], blk["off"], blk["b"]
                      kb = bankA()
                      MM(ps[0:nb, kb, :], lrT[0:17, off:off + nb], wg[0:17, :], True, True, ["lrT", "wg"], ["ps%d" % kb])
                      ACT(spt[0:nb, :], ps[0:nb, kb, :], AF.Exp, ["ps%d" % kb], ["tmpf1"], scale=-1.0)
                      ACT(spt[0:nb, :], spt[0:nb, :], AF.Ln, ["tmpf1"], ["tmpf1"], bias=1.0)
                      kb = bankA()
                      pv = ps[:, kb, :].rearrange("p (h t) -> p h t", h=NH)
                      for h in range(NH):
                          MM(pv[:, h, 0:nb], spt[0:nb, h * 128:(h + 1) * 128], triuf[0:nb, 0:nb], True, True,
                             ["tmpf1", "triuf"], ["ps%d" % kb])
                      ACT(Ee[:, 0, :, 0:nb], pv[:, :, 0:nb], AF.Exp, ["ps%d" % kb], ["Ee"], scale=-1.0 / 16)
                      ACT(Ee[:, 1, :, 0:nb], pv[:, :, 0:nb], AF.Exp, ["ps%d" % kb], ["Ee"], scale=1.0 / 16)
                      TT_("dve", qk[:, 0, :, off:off + nb], qk[:, 0, :, off:off + nb], Ee[:, 0, :, 0:nb], ALU.mult, ["qk", "Ee"], ["qk"])
                      TT_("dve", qk[:, 1, :, off:off + nb], qk[:, 1, :, off:off + nb], Ee[:, 1, :, 0:nb], ALU.mult, ["qk", "Ee"], ["qk"])
                      for h in range(NH):
                          TS("dve", khT[:, h, 0:nb], qk[:, 1, h, off:off + nb], Ee[:, 0, h, nb - 1:nb], None, ALU.mult, None,
                             ["qk", "Ee"], ["khT"])
                      kb = bankA()
                      pb = psbf(kb)
                      for h in range(NH):
                          TR(pb[0:nb, h * 128:(h + 1) * 128], khT[:, h, 0:nb], idb[:, :], ["khT", "idb"], ["ps%d" % kb])
                      CP("act", khat[0:nb, :], pb[0:nb, 0:512], ["ps%d" % kb], ["khat"])
                      k2 = bankB()
                      for h in range(NH):
                          MM(ps[:, k2 + h // 2, (h % 2) * 256:(h % 2 + 1) * 256], khat[0:nb, h * 128:(h + 1) * 128],
                             vv[0:nb, b, h * 256:(h + 1) * 256], True, True, ["khat", "vv%d" % b], ["ps%d" % k2, "ps%d" % (k2 + 1)])
                      return k2

                  def kv_h(k2, h):
                      return ps[:, k2 + h // 2, (h % 2) * 256:(h % 2 + 1) * 256]

                  def gla_out(blk, sin_res):
                      nb, off, b = blk["nb"], blk["off"], blk["b"]
                      kb = bankA()
                      pv = ps[:, kb, :].rearrange("p (h t) -> p h t", h=NH)
                      for h in range(NH):
                          MM(pv[0:nb, h, 0:nb], qk[:, 1, h, off:off + nb], qk[:, 0, h, off:off + nb], True, True, ["qk"], ["ps%d" % kb])
                      if DEBUG_SUB < 2:
                          return
                      TT_("dve", scm[0:nb, :, 0:nb], pv[0:nb, :, 0:nb], maskb[0:nb, :, 0:nb], ALU.mult, ["ps%d" % kb, "maskb"], ["scm"])
                      if DEBUG_SUB < 3:
                          return
                      k2 = bankB()
                      for h in range(NH):
                          oh = ps[0:nb, k2 + h // 2, (h % 2) * 256:(h % 2 + 1) * 256]
                          MM(oh, scm[0:nb, h, 0:nb], vv[0:nb, b, h * 256:(h + 1) * 256], True, False, ["scm", "vv%d" % b],
                             ["ps%d" % k2, "ps%d" % (k2 + 1)])
                          MM(oh, qk[:, 0, h, off:off + nb], Sinbf[:, h * 256:(h + 1) * 256], False, True, ["qk"] + sin_res,
                             ["ps%d" % k2, "ps%d" % (k2 + 1)])
                      if DEBUG_SUB < 4:
                          return
                      for h in range(NH):
                          oh = ps[0:nb, k2 + h // 2, (h % 2) * 256:(h % 2 + 1) * 256]
                          ACT(junk[0:nb, 0:256], oh, AF.Square, ["ps%d" % k2, "ps%d" % (k2 + 1)], ["junk", "sm1"], accum_out=sm[0:nb, 4 + h:5 + h])
                      if DEBUG_SUB < 5:
                          return
                      rsqrt_(sm[0:nb, 4:8], nb, 1.0 / DV, ["sm1"])
                      if DEBUG_SUB < 6:
                          return
                      for h in range(NH):
                          oh = ps[0:nb, k2 + h // 2, (h % 2) * 256:(h % 2 + 1) * 256]
                          STT("dve", Mt[0:nb, b, h * 256:(h + 1) * 256], oh, sm[0:nb, 4 + h:5 + h], Mt[0:nb, b, h * 256:(h + 1) * 256],
                              ALU.mult, ALU.mult, ["ps%d" % k2, "ps%d" % (k2 + 1), "sm1", "Mt%d" % b], ["Mt%d" % b])

                  def evac_ga(half):
                      def f(blk, kb):
                          nb, b = blk["nb"], blk["b"]
                          tq = b % 2
                          ACT(tmpf[0:nb, tq, :], ps[0:nb, kb, :], AF.Exp, ["ps%d" % kb], ["tmpf%d" % tq], scale=-1.0)
                          ACT(tmpf[0:nb, tq, :], tmpf[0:nb, tq, :], AF.Ln, ["tmpf%d" % tq], ["tmpf%d" % tq], bias=1.0)
                          ACT(tmpf[0:nb, tq, :], tmpf[0:nb, tq, :], AF.Exp, ["tmpf%d" % tq], ["tmpf%d" % tq], scale=-1.0)
                          def later():
                              TT_("dve", tmpf[0:nb, tq, :], tmpf[0:nb, tq, :], ps[0:nb, kb, :], ALU.mult, ["tmpf%d" % tq, "ps%d" % kb], ["tmpf%d" % tq])
                              TT_("dve", Mt[0:nb, b, half * 512:(half + 1) * 512], tmpf[0:nb, tq, :], g2x2[0:nb, :], ALU.mult,
                                  ["tmpf%d" % tq, "g2x2"], ["Mt%d" % b])
                          return later
                      return f

                  _stage(5)
                  if r == 0:
                      sblk = blocks[-1]
                      nb = SSQ
                      k2 = gla_prep(sblk)
                      CP("act", Sinbf[:, :], Ssamp[:, :], ["Ssamp"], ["Sinbf"])
                      for h in range(NH):
                          STT("dve", Ssamp[:, h * 256:(h + 1) * 256], Ssamp[:, h * 256:(h + 1) * 256], Ee[:, 0, h, nb - 1:nb], kv_h(k2, h),
                              ALU.mult, ALU.add, ["Ssamp", "Ee", "ps%d" % k2, "ps%d" % (k2 + 1)], ["Ssamp"])
                      DMA("sp", "gss", gss[l].rearrange("h d v -> d h v"), Ssamp[:, :].rearrange("p (h v) -> p h v", h=NH), ["Ssamp"], [])
                  _stage(6)
                  for blk in pblocks:
                      b, nb = blk["b"], blk["nb"]
                      k2 = gla_prep(blk)
                      if b == 0:
                          for hh in range(2):
                              CP("dve", Sloc[:, hh * 512:(hh + 1) * 512], ps[:, k2 + hh, :], ["ps%d" % (k2 + hh)], ["Sloc"])
                      else:
                          CP("act", Slocbf[:, b - 1, :], Sloc[:, :], ["Sloc"], ["Slocbf%d" % b])
                          for h in range(NH):
                              STT("dve", Sloc[:, h * 256:(h + 1) * 256], Sloc[:, h * 256:(h + 1) * 256], Ee[:, 0, h, nb - 1:nb], kv_h(k2, h),
                                  ALU.mult, ALU.add, ["Sloc", "Ee", "ps%d" % k2, "ps%d" % (k2 + 1)], ["Sloc"])
                      TT_("dve", Pall[:, b + 1, :], Pall[:, b, :], Ee[:, 0, :, nb - 1], ALU.mult, ["Pall", "Ee"], ["Pall"])

                  _stage(7)
                  par = cc_count[0] % 2
                  DMA("act", "cci", cc_in[par].ap()[:, 0:1024], Sloc[:, :], ["Sloc"], ["cc_in%d" % par])
                  DMA("act", "cci", cc_in[par].ap()[:, 1024:1028], Pall[:, NBP, :], ["Pall"], ["cc_in%d" % par])
                  cin, cout = cc_in[par], cc_out[par]
                  S.dma("pool", "cc%d" % cc_count[0],
                        lambda e, cin=cin, cout=cout: e.collective_compute("AllGather", ALU.bypass, replica_groups=[list(range(NCORES))],
                                                                           ins=[cin.ap().opt()], outs=[cout.ap().opt()]),
                        ["cc_in%d" % par], ["cc_out%d" % par, "ccbar"], inc=1)
                  cc_count[0] += 1
                  if DEBUG_STAGE == 7:
                      for j in range(NCORES):
                          DMA("sp", "gt0", Gt2[:, 0, :], cc_out[par].ap()[j * 128:(j + 1) * 128, :], ["cc_out%d" % par], ["Gt0"])

                  _stage(8)
                  form1(4, evac_ga(0))
                  form1(5, evac_ga(1))

                  _stage(9)
                  if r == 0:
                      gla_out(blocks[-1], ["Sinbf"])

                  _stage(10)
                  def evac_vm(half):
                      def f(blk, kb):
                          nb, b = blk["nb"], blk["b"]
                          tq = b % 2
                          sb_ = 8 if tq == 0 else 18
                          t3 = 16 if tq == 0 else 26
                          r2, r3 = "sm2_%d" % tq, "sm3_%d" % tq
                          for gi in range(2):
                              g = 2 * half + gi
                              pg = ps[0:nb, kb, gi * 256:(gi + 1) * 256]
                              ACT(junk[0:nb, 0:256], pg, AF.Identity, ["ps%d" % kb], ["junk", r2], accum_out=sm[0:nb, sb_ + g:sb_ + 1 + g])
                              ACT(junk[0:nb, 0:256], pg, AF.Square, ["ps%d" % kb], ["junk", r2], accum_out=sm[0:nb, sb_ + 4 + g:sb_ + 5 + g])
                          def later():
                              c0, c1 = sb_ + 2 * half, sb_ + 2 + 2 * half
                              d0, d1 = sb_ + 4 + 2 * half, sb_ + 6 + 2 * half
                              TS("dve", sm[0:nb, c0:c1], sm[0:nb, c0:c1], 1.0 / 256, None, ALU.mult, None, [r2], [r2])
                              TT_("dve", sm[0:nb, t3:t3 + 2], sm[0:nb, c0:c1], sm[0:nb, c0:c1], ALU.mult, [r2], [r3])
                              STT("dve", sm[0:nb, d0:d1], sm[0:nb, d0:d1], 1.0 / 256, sm[0:nb, t3:t3 + 2], ALU.mult, ALU.subtract, [r2, r3], [r2])
                              rsqrt_(sm[0:nb, d0:d1], nb, 1.0, [r2])
                              for gi in range(2):
                                  g = 2 * half + gi
                                  pg = ps[0:nb, kb, gi * 256:(gi + 1) * 256]
                                  tg = tmpf[0:nb, tq, gi * 256:(gi + 1) * 256]
                                  STT("dve", tg, pg, sm[0:nb, sb_ + g:sb_ + 1 + g], lnp[0:nb, 0, g * 256:(g + 1) * 256], ALU.subtract, ALU.mult,
                                      ["ps%d" % kb, r2, "lnp"], ["tmpf%d" % tq])
                                  if blk["samp"]:
                                      STT("dve", Ssamp[0:nb, g * 256:(g + 1) * 256], tg, sm[0:nb, sb_ + 4 + g:sb_ + 5 + g], lnp[0:nb, 1, g * 256:(g + 1) * 256],
                                          ALU.mult, ALU.add, ["tmpf%d" % tq, r2, "lnp"], ["Ssamp"])
                                      CP("act", vmn[0:nb, b, g * 256:(g + 1) * 256], Ssamp[0:nb, g * 256:(g + 1) * 256], ["Ssamp"], ["vmn%d" % b])
                                  else:
                                      STT("dve", vmn[0:nb, b, g * 256:(g + 1) * 256], tg, sm[0:nb, sb_ + 4 + g:sb_ + 5 + g], lnp[0:nb, 1, g * 256:(g + 1) * 256],
                                          ALU.mult, ALU.add, ["tmpf%d" % tq, r2, "lnp"], ["vmn%d" % b])
                              if half == 1:
                                  if blk["samp"]:
                                      DMA("sp", "mvo", mvo[l], Ssamp[0:nb, :], ["Ssamp"], [])
                                  k2 = bankB()
                                  for g in range(4):
                                      MM(ps[0:nb, k2 + g // 2, (g % 2) * 256:(g % 2 + 1) * 256], wsT[0:nb, g, 0:nb], vmn[0:nb, b, g * 256:(g + 1) * 256],
                                         True, True, ["wsT", "vmn%d" % b], ["ps%d" % k2, "ps%d" % (k2 + 1)])
                                  for g in range(4):
                                      ACT(vmn[0:nb, b, g * 256:(g + 1) * 256], ps[0:nb, k2 + g // 2, (g % 2) * 256:(g % 2 + 1) * 256], AF.Identity,
                                          ["ps%d" % k2, "ps%d" % (k2 + 1), "bsT"], ["vmn%d" % b], bias=bsT[0:nb, g:g + 1])
                          return later
                      return f
                  form1(6, evac_vm(0))
                  form1(7, evac_vm(1))

                  _stage(11)
                  def fold(js):
                      for j in js:
                          Gt = Gt2[:, j % 2, :]
                          gres = "Gt%d" % (j % 2)
                          DMA("sp", "gt%d" % (j % 2), Gt, cc_out[par].ap()[j * 128:(j + 1) * 128, :], ["cc_out%d" % par], [gres])
                          if j == 0:
                              TS("dve", Sstart[:, :], Scarry[:, :], oneh[:, 0:1], None, ALU.mult, None, ["Scarry", "oneh"], ["Sstart"])
                          else:
                              STT("dve", Sstart[:, :], Scarry[:, :], oneh[:, j:j + 1], Sstart[:, :], ALU.mult, ALU.add,
                                  ["Scarry", "oneh", "Sstart"], ["Sstart"])
                          for h in range(NH):
                              STT("dve", Scarry[:, h * 256:(h + 1) * 256], Scarry[:, h * 256:(h + 1) * 256], Gt[:, 1024 + h:1025 + h],
                                  Gt[:, h * 256:(h + 1) * 256], ALU.mult, ALU.add, ["Scarry", gres], ["Scarry"])
                  fold(range(0, 4))
                  _stage(12)
                  def evac_u(half):
                      def f(blk, kb):
                          nb, b = blk["nb"], blk["b"]
                          sl = vmn[0:nb, b, half * 512:(half + 1) * 512]
                          TT_("dve", sl, ps[0:nb, kb, :], sl, ALU.mult, ["ps%d" % kb, "vmn%d" % b], ["vmn%d" % b])
                      return f
                  form1(8, evac_u(0))
                  fold(range(4, NCORES))
                  if r == R - 1:
                      DMA("sp", "gsp", gsp[l].rearrange("h d v -> d h v"), Scarry[:, :].rearrange("p (h v) -> p h v", h=NH), ["Scarry"], [])

                  form1(9, evac_u(1))

                  _stage(14)
                  def evac_gb(half):
                      def f(blk, kb):
                          nb, b = blk["nb"], blk["b"]
                          tq = b % 2
                          ACT(tmpf[0:nb, tq, :], ps[0:nb, kb, :], AF.Exp, ["ps%d" % kb], ["tmpf%d" % tq], scale=-1.0)
                          ACT(tmpf[0:nb, tq, :], tmpf[0:nb, tq, :], AF.Ln, ["tmpf%d" % tq], ["tmpf%d" % tq], bias=1.0)
                          ACT(tmpf[0:nb, tq, :], tmpf[0:nb, tq, :], AF.Exp, ["tmpf%d" % tq], ["tmpf%d" % tq], scale=-1.0)
                          def later():
                              TT_("dve", tmpf[0:nb, tq, :], tmpf[0:nb, tq, :], ps[0:nb, kb, :], ALU.mult, ["tmpf%d" % tq, "ps%d" % kb], ["tmpf%d" % tq])
                              sl = vmn[0:nb, b, half * 512:(half + 1) * 512]
                              TT_("dve", sl, sl, tmpf[0:nb, tq, :], ALU.mult, ["tmpf%d" % tq, "vmn%d" % b], ["vmn%d" % b])
                          return later
                      return f
                  form1(10, evac_gb(0))
                  _stage(13)
                  def gla_prompt(blks):
                      for blk in blks:
                          b = blk["b"]
                          if b == 0:
                              CP("act", Sinbf[:, :], Sstart[:, :], ["Sstart"], ["Sinbf"])
                          else:
                              for h in range(NH):
                                  STT("dve", Sinbf[:, h * 256:(h + 1) * 256], Sstart[:, h * 256:(h + 1) * 256], Pall[:, b, h:h + 1],
                                      Slocbf[:, b - 1, h * 256:(h + 1) * 256], ALU.mult, ALU.add, ["Sstart", "Pall", "Slocbf%d" % b], ["Sinbf"])
                          gla_out(blk, ["Sinbf"])
                  gla_prompt(pblocks[:2])
                  form1(11, evac_gb(1))
                  gla_prompt(pblocks[2:])

                  _stage(15)
                  def transposes(half):
                      for blk in blocks:
                          nb, off, b = blk["nb"], blk["off"], blk["b"]
                          kb = bankA()
                          pv = psbf(kb).rearrange("p (j t) -> p j t", j=8)
                          srct = Mt if half == 0 else vmn
                          sres = ("Mt%d" if half == 0 else "vmn%d") % b
                          for j in range(8):
                              TR(pv[:, j, 0:nb], srct[0:nb, b, j * 128:(j + 1) * 128], idb[0:nb, 0:nb], [sres, "idb"], ["ps%d" % kb])
                          CP("act" if half == 0 else "dve", oT[:, half * 8:(half + 1) * 8, off:off + nb], pv[:, :, 0:nb], ["ps%d" % kb], ["oT%d_%d" % (b, half)])
                  transposes(1)
                  ti = tiles.index((l, r))
                  if ti + 1 < len(tiles) and (DEBUG_STAGE is None) and (R > 1 or tiles[ti + 1][0] == l):
                      doneA.add(tiles[ti + 1])
                      phaseA(*tiles[ti + 1])
                  transposes(0)
                  _stage(16)
                  steps = [(jq, blk) for jq in range(4) for blk in blocks]
                  NSLOT = 8; AHEAD = 5; BEHIND = 2

                  def slot_of(si):
                      xq = si % NSLOT
                      jq, blk = steps[si]
                      nb = blk["nb"]
                      return xt[0:nb, xq // 4, (xq % 4) * 512:(xq % 4 + 1) * 512], "xtq%d" % xq, "xq%d" % xq

                  def ld(si):
                      jq, blk = steps[si]
                      slot, sres_, lane = slot_of(si)
                      DMA("sp", lane, slot, blk["src"][:, jq * 512:(jq + 1) * 512], [blk["res"] + "_%d" % jq], [sres_])

                  def stq(si):
                      jq, blk = steps[si]
                      slot, sres_, lane = slot_of(si)
                      DMA("sp", lane, blk["dst"][:, jq * 512:(jq + 1) * 512], slot, [sres_], [blk["res"] + "_%d" % jq])

                  for si in range(min(AHEAD, len(steps))):
                      ld(si)
                  for si, (jq, blk) in enumerate(steps):
                      k = get_slab(12 + jq)
                      nb, off, b = blk["nb"], blk["off"], blk["b"]
                      if si + AHEAD < len(steps):
                          ld(si + AHEAD)
                      kb = bankA()
                      for c in range(16):
                          MM(ps[0:nb, kb, :], oT[:, c, off:off + nb], slab[:, k, c, :], c == 0, c == 15, ["oT%d_0" % b, "oT%d_1" % b, "slab%d_%d" % (k, c // 2)], ["ps%d" % kb])
                      slot, sres_, lane = slot_of(si)
                      TT_("dve", slot, slot, ps[0:nb, kb, :], ALU.add, [sres_, "ps%d" % kb], [sres_])
                      if si - BEHIND >= 0:
                          stq(si - BEHIND)
                  for si in range(max(0, len(steps) - BEHIND), len(steps)):
                      stq(si)

        except _Stop:
            pass
        _final = DEBUG_STAGE is None or DEBUG_STAGE >= 17
        gfin = lnp[:, :, :].rearrange("p a b -> p (a b)")
        DMA("sp", "p2", gfin, fin_g.partition_broadcast(128), [], ["lnp"])
        fb = []
        for r in (range(R) if _final else []):
            for j in range(NBP):
                fb.append((128, resid_p[r, j * 128:(j + 1) * 128, :], yp[r, j * 128:(j + 1) * 128, :], "res_%d_%d" % (r, j)))
        if _final:
            fb.append((SSQ, resid_s, ys, "res_s"))
        for i, (nb, src, dst, res) in enumerate(fb):
            k = i % 2
            xres = ["xtq%d" % (4 * k + q) for q in range(4)]
            DMA("sp", "xt%d" % k, xt[0:nb, k, :], src, [res + "_%d" % q for q in range(4)], xres)
            ACT(xn[0:nb, :], xt[0:nb, k, :], AF.Square, xres, ["xn", "sm0"], accum_out=sm[0:nb, 0:1])
            rsqrt_(sm[0:nb, 0:1], nb, 1.0 / D, ["sm0"])
            STT("dve", xt[0:nb, k, :], xt[0:nb, k, :], sm[0:nb, 0:1], gfin[0:nb, :], ALU.mult, ALU.mult, xres + ["sm0", "lnp"], xres)
            DMA("sp", "xt%d" % k, dst, xt[0:nb, k, :], xres, [])

        print("sbuf bytes remaining", nc.sbuf_bytes_remaining, "ops", len(S.ops))
        S.emit(nc)
    return nc


def run(inputs, L, R):
    f = lambda a: np.ascontiguousarray(np.asarray(a, dtype=np.float32))
    x_prompt = f(inputs["x_prompt"]); x_sample = f(inputs["x_sample"]); state = f(inputs["state_gla"])
    nc = build(L, R)
    xpr = x_prompt.reshape(R, NCORES, TP, D)
    ident = np.eye(128, dtype=np.float32)
    triu = np.triu(np.ones((128, 128), dtype=np.float32))
    shared = {k: f(inputs[k])[:L] for k in ("w_in", "w_out", "w_gate_up", "b_gate", "w_s", "b_s", "norm_g", "gla_norm_g",
                                            "mlp_ln_g", "mlp_ln_b")}
    shared["final_norm_g"] = f(inputs["final_norm_g"])
    in_maps = []
    for i in range(NCORES):
        oh = np.zeros((128, NCORES), np.float32); oh[:, i] = 1.0
        m = dict(shared)
        m.update(xp=np.ascontiguousarray(xpr[:, i]), xs=np.ascontiguousarray(x_sample[i]), sg=np.ascontiguousarray(state[:L, i]),
                 ident=ident, triu=triu, onehot=oh)
        in_maps.append(m)
    res = run_bass_kernel_spmd(nc, in_maps, core_ids=list(range(NCORES)))
    rs = res.results
    y_prompt = np.stack([rs[i]["yp"] for i in range(NCORES)], axis=1).reshape(1, R * NCORES * TP, D)
    y_sample = np.stack([rs[i]["ys"] for i in range(NCORES)], axis=0)
    gla_p = rs[0]["gsp"][:, None]
    gla_s = np.stack([rs[i]["gss"] for i in range(NCORES)], axis=1)
    mlp_v = np.stack([rs[i]["mv"] for i in range(NCORES)], axis=1)
    return (y_prompt.astype(np.float32), y_sample.astype(np.float32), gla_p.astype(np.float32),
            gla_s.astype(np.float32), mlp_v.astype(np.float32))


def kernel(**inputs):
    return run(inputs, 4, 4)
```
